# Optimizing a Trainium2 kernel written in Bass

```python
import math
import jax, jax.numpy as jnp
from jax import lax
import numpy as np

D_MODEL = 1024
BATCH = 8
SEQ = 4096
DEPTH = 4

N_A = DEPTH // 2
N_B = DEPTH - N_A
POOL_WINDOWS = (2, 4, 8, 16)
N_POOL_GROUPS = len(POOL_WINDOWS)
GROUP_CH = D_MODEL // N_POOL_GROUPS
HEAD_DIM = 64
N_Q_HEADS = D_MODEL // HEAD_DIM
N_KV_HEADS = 4
Q_PER_KV = N_Q_HEADS // N_KV_HEADS
WINDOW = 128
BLOCK = 128
ROPE_THETA = 10000.0
ATTN_SCALE = 1.0 / math.sqrt(HEAD_DIM)
NEG_INF = -1e30
D_FF = 2816
CONV_WIDTH = 3
RMS_EPS = 1e-6

kernel_name = "yoco_pool_swa_sink_hybrid"


def rms_norm(x, g):
    xf = x.astype(jnp.float32)
    y = xf * lax.rsqrt(jnp.mean(xf * xf, axis=-1, keepdims=True) + RMS_EPS)
    return (y * g.astype(jnp.float32)).astype(x.dtype)


def pool_mixer(h, w_pool, scale):
    B, S, D = h.shape
    hf = h.astype(jnp.float32)
    csum = jnp.concatenate([jnp.zeros((B, 1, D), jnp.float32), jnp.cumsum(hf, axis=1)], axis=1)
    t = jnp.arange(1, S + 1)
    diffs = []
    for gi, w in enumerate(POOL_WINDOWS):
        sl = slice(gi * GROUP_CH, (gi + 1) * GROUP_CH)
        lo = jnp.maximum(t - w, 0)
        cnt = jnp.minimum(t, w).astype(jnp.float32)
        mean = (csum[:, 1:, sl] - csum[:, lo, sl]) / cnt[None, :, None]
        diffs.append(mean - hf[..., sl])
    d = jnp.stack(diffs, axis=2).astype(h.dtype)
    y = jnp.einsum('bsgc,gcd->bsgd', d, w_pool).reshape(B, S, D)
    return y * scale


def conv_glu_ffn(h, w_in, conv_w, conv_b, w_out):
    S = h.shape[1]
    u = h @ w_in
    up = jnp.pad(u, ((0, 0), (CONV_WIDTH - 1, 0), (0, 0)))
    u = sum(conv_w[k] * up[:, k:k + S] for k in range(CONV_WIDTH)) + conv_b
    gate, val = jnp.split(u, 2, axis=-1)
    return (jax.nn.gelu(gate, approximate=True) * val) @ w_out


def rope(x, cos, sin):
    xf = x.astype(jnp.float32)
    x1, x2 = jnp.split(xf, 2, axis=-1)
    return jnp.concatenate([x1 * cos - x2 * sin, x2 * cos + x1 * sin], axis=-1).astype(x.dtype)


def rope_tables(positions):
    inv_freq = 1.0 / (ROPE_THETA ** (jnp.arange(0, HEAD_DIM, 2, dtype=jnp.float32) / HEAD_DIM))
    ang = positions.astype(jnp.float32)[..., None] * inv_freq
    return jnp.cos(ang)[:, :, None, :], jnp.sin(ang)[:, :, None, :]


def band_blocks(t):
    B, S = t.shape[:2]
    nb = S // BLOCK
    tb = t.reshape(B, nb, BLOCK, N_KV_HEADS, HEAD_DIM)
    prev = jnp.pad(tb, ((0, 0), (1, 0), (0, 0), (0, 0), (0, 0)))[:, :-1]
    return jnp.concatenate([prev, tb], axis=2).astype(jnp.float32)


def swa_sink_attention(q, kk, vv, sinks):
    B, S = q.shape[:2]
    nb = S // BLOCK
    qb = q.reshape(B, nb, BLOCK, N_KV_HEADS, Q_PER_KV, HEAD_DIM).astype(jnp.float32) * ATTN_SCALE
    s = jnp.einsum('bnqhgd,bnkhd->bnhgqk', qb, kk)
    qi = jnp.arange(BLOCK)[:, None]
    kj = jnp.arange(2 * BLOCK)[None, :]
    rel = BLOCK + qi - kj
    blk = jnp.arange(nb)[:, None, None]
    valid = (rel >= 0) & (rel < WINDOW) & (blk * BLOCK + kj - BLOCK >= 0)
    s = jnp.where(valid[None, :, None, None], s, NEG_INF)
    sink = sinks.astype(jnp.float32).reshape(N_KV_HEADS, Q_PER_KV)[None, None, :, :, None, None]
    m = jnp.maximum(jnp.max(s, axis=-1, keepdims=True), sink)
    p = jnp.exp(s - m)
    denom = jnp.sum(p, axis=-1) + jnp.exp(sink - m)[..., 0]
    o = jnp.einsum('bnhgqk,bnkhd->bnhgqd', p, vv) / denom[..., None]
    o = o.transpose(0, 1, 4, 2, 3, 5).reshape(B, S, N_Q_HEADS * HEAD_DIM)
    return o.astype(q.dtype)


def setup_inputs(seed: int = 0) -> dict:
    key = jax.random.key(seed)
    ks = jax.random.split(key, 20)
    f32 = jnp.float32
    D, F = D_MODEL, D_FF
    HQD, HKVD = N_Q_HEADS * HEAD_DIM, N_KV_HEADS * HEAD_DIM

    def gain(k, shape):
        return 1.0 + 0.05 * jax.random.normal(k, shape, f32)

    x = jax.random.normal(ks[0], (BATCH, SEQ, D), f32)
    positions = jnp.broadcast_to(jnp.arange(SEQ, dtype=jnp.int32)[None, :], (BATCH, SEQ))
    return {
        "x": x,
        "positions": positions,
        "mix_pre_g": gain(ks[1], (DEPTH, D)),
        "mix_post_g": gain(ks[2], (DEPTH, D)),
        "pool_w": jax.random.normal(ks[3], (N_A, N_POOL_GROUPS, GROUP_CH, GROUP_CH), f32) * GROUP_CH ** -0.5,
        "pool_scale": 1.0 + 0.1 * jax.random.normal(ks[4], (N_A, D), f32),
        "kv_norm_g": gain(ks[5], (D,)),
        "w_kv": jax.random.normal(ks[6], (D, 2 * HKVD), f32) * D ** -0.5,
        "w_q": jax.random.normal(ks[7], (N_B, D, HQD), f32) * D ** -0.5,
        "w_o": jax.random.normal(ks[8], (N_B, HQD, D), f32) * HQD ** -0.5,
        "sinks": jax.random.normal(ks[9], (N_B, N_Q_HEADS), f32),
        "ffn_pre_g": gain(ks[10], (DEPTH, D)),
        "ffn_post_g": gain(ks[11], (DEPTH, D)),
        "ffn_w_in": jax.random.normal(ks[12], (DEPTH, D, 2 * F), f32) * D ** -0.5,
        "ffn_conv_w": jax.random.normal(ks[13], (DEPTH, CONV_WIDTH, 2 * F), f32) * CONV_WIDTH ** -0.5,
        "ffn_conv_b": 0.01 * jax.random.normal(ks[14], (DEPTH, 2 * F), f32),
        "ffn_w_out": jax.random.normal(ks[15], (DEPTH, F, D), f32) * F ** -0.5,
    }


def reference(x, positions, mix_pre_g, mix_post_g, pool_w, pool_scale, kv_norm_g, w_kv,
              w_q, w_o, sinks, ffn_pre_g, ffn_post_g, ffn_w_in, ffn_conv_w, ffn_conv_b, ffn_w_out):
    B, S, D = x.shape
    cos, sin = rope_tables(positions)
    kk = vv = None
    for layer in range(DEPTH):
        h = rms_norm(x, mix_pre_g[layer])
        if layer < N_A:
            m = pool_mixer(h, pool_w[layer], pool_scale[layer])
        else:
            if layer == N_A:
                hkv = rms_norm(x, kv_norm_g)
                kv = (hkv @ w_kv).reshape(B, S, 2, N_KV_HEADS, HEAD_DIM)
                k_shared = rope(kv[:, :, 0], cos, sin)
                kk, vv = band_blocks(k_shared), band_blocks(kv[:, :, 1])
            j = layer - N_A
            q = rope((h @ w_q[j]).reshape(B, S, N_Q_HEADS, HEAD_DIM), cos, sin)
            m = swa_sink_attention(q, kk, vv, sinks[j]) @ w_o[j]
        x = x + rms_norm(m, mix_post_g[layer])
        f = conv_glu_ffn(rms_norm(x, ffn_pre_g[layer]), ffn_w_in[layer], ffn_conv_w[layer],
                         ffn_conv_b[layer], ffn_w_out[layer])
        x = x + rms_norm(f, ffn_post_g[layer])
    return x
```

```python
import math
from contextlib import ExitStack

import numpy as np
import ml_dtypes

import concourse.bass as bass
import concourse.mybir as mybir
from concourse.bass_utils import run_bass_kernel_spmd

F32 = mybir.dt.float32
BF16 = mybir.dt.bfloat16
I32 = mybir.dt.int32
AF = mybir.ActivationFunctionType
ALU = mybir.AluOpType

DM = 1024
SEQ = 4096
NCORE = 8
T = 512
NCH = SEQ // T
KD = DM // 128
FF = 2816
NFP = FF // 128
NSLOT = 5
SLABW = 4096
EPS = 1e-6
NL_RUN = 4
NCH_RUN = NCH
DBG_LEVEL = 9

GV0 = 0
CW0 = GV0 + 19 * 8
SK0 = CW0 + 4 * 4 * 2 * NFP
IF0 = SK0 + 16
IC0 = IF0 + 32
EP0 = IC0 + 64
CFW = 976
ID0, ON0, OA0, OB0, MO0, MP0, MPO0 = 0, 128, 256, 384, 512, 1024, 1536
CBW = 2048
POOL_W = (2, 4, 8, 16)


def slab_plan():
    plan = []
    for l in range(4):
        if l < 2:
            plan.append((f"pool{l}", 2048))
        else:
            if l == 2:
                plan += [("kvk", 4096), ("kvv", 2048)]
            plan += [(f"wq{l}_0", 4096), (f"wq{l}_1", 4096), (f"wo{l}_0", 4096), (f"wo{l}_1", 4096)]
        plan += [(f"win{l}_{s}", 4096) for s in range(11)]
        plan += [(f"wout{l}_{m}", 2816) for m in range(8)]
    offs = {}
    o = 0
    for k, n in plan:
        offs[k] = (o, n)
        o += n
    return plan, offs, o


PLAN, OFFS, TOT = slab_plan()


def conv_ranges():
    rngs = []
    cur0, cur1 = 0, 0
    for k, n in PLAN:
        o, _ = OFFS[k]
        if cur1 - cur0 + n > 14336:
            rngs.append((cur0, cur1))
            cur0 = cur1
        cur1 = o + n
    rngs.append((cur0, cur1))
    return rngs


CRANGES = conv_ranges()


def range_of(key):
    o, n = OFFS[key]
    for i, (a, b) in enumerate(CRANGES):
        if a <= o and o + n <= b:
            return i
    raise AssertionError(key)


def pack_weights(inp):
    W = np.zeros((128, TOT), np.float32)

    def put(key, arr):
        o, n = OFFS[key]
        W[:, o:o + n] = np.ascontiguousarray(arr).reshape(128, n)

    for l in range(4):
        if l < 2:
            pw = np.asarray(inp["pool_w"][l])
            put(f"pool{l}", pw.reshape(4, 2, 128, 256).transpose(2, 0, 1, 3))
        else:
            j = l - 2
            if l == 2:
                wkv = np.asarray(inp["w_kv"])
                wk = wkv[:, :256].reshape(8, 128, 4, 64)
                kd = np.broadcast_to(wk[:, :, :, None, :], (8, 128, 4, 2, 64))
                put("kvk", kd.transpose(1, 0, 2, 3, 4))
                put("kvv", wkv[:, 256:].reshape(8, 128, 256).transpose(1, 0, 2))
            wq = np.asarray(inp["w_q"][j]).reshape(8, 128, 2, 512)
            wo = np.asarray(inp["w_o"][j]).reshape(8, 128, 2, 512)
            for hf in range(2):
                put(f"wq{l}_{hf}", wq[:, :, hf, :].transpose(1, 0, 2))
                put(f"wo{l}_{hf}", wo[:, :, hf, :].transpose(1, 0, 2))
        win = np.asarray(inp["ffn_w_in"][l]).reshape(8, 128, 2, NFP, 128)
        for s in range(11):
            put(f"win{l}_{s}", win[:, :, :, 2 * s:2 * s + 2, :].transpose(1, 3, 2, 0, 4))
        wout = np.asarray(inp["ffn_w_out"][l]).reshape(NFP, 128, 8, 128)
        for m in range(8):
            put(f"wout{l}_{m}", wout[:, :, m, :].transpose(1, 0, 2))
    return W


def pack_consts(inp):
    cf = np.zeros((128, CFW), np.float32)

    def col(v):
        return np.asarray(v, np.float32).reshape(8, 128).T

    vecs = [inp["mix_pre_g"][l] for l in range(4)] + [inp["mix_post_g"][l] for l in range(4)] \
        + [inp["ffn_pre_g"][l] for l in range(4)] + [inp["ffn_post_g"][l] for l in range(4)] \
        + [inp["kv_norm_g"]] + [inp["pool_scale"][l] for l in range(2)]
    for i, v in enumerate(vecs):
        cf[:, GV0 + i * 8:GV0 + i * 8 + 8] = col(v)
    cwv = np.asarray(inp["ffn_conv_w"], np.float32)
    cbv = np.asarray(inp["ffn_conv_b"], np.float32)
    for l in range(4):
        for q in range(4):
            v = cwv[l, q] if q < 3 else cbv[l]
            for half in range(2):
                c0 = CW0 + ((l * 4 + q) * 2 + half) * NFP
                cf[:, c0:c0 + NFP] = v[half * FF:(half + 1) * FF].reshape(NFP, 128).T
    sk = np.asarray(inp["sinks"], np.float32)
    for j in range(2):
        for m in range(8):
            cf[0:64, SK0 + j * 8 + m] = sk[j, 2 * m]
            cf[64:128, SK0 + j * 8 + m] = sk[j, 2 * m + 1]
    invf = (1.0 / (10000.0 ** (np.arange(0, 64, 2, dtype=np.float32) / 64.0))).astype(np.float32)
    cf[:, IF0:IF0 + 32] = invf[None, :]
    for g, w in enumerate(POOL_W):
        cf[:, IC0 + g * 16:IC0 + g * 16 + 16] = (1.0 / np.minimum(np.arange(1, 17), w)).astype(np.float32)[None, :]
    cf[:, EP0] = EPS
    cb = np.zeros((128, CBW), np.float32)
    cb[:, ID0:ID0 + 128] = np.eye(128)
    cb[:, ON0:ON0 + 128] = 1.0
    cb[:, OA0:OA0 + 64] = 1.0
    cb[:, OB0 + 64:OB0 + 128] = 1.0
    kk = np.arange(128)[:, None]
    qq = np.arange(128)[None, :]
    cb[:, MO0:MO0 + 512] = np.tile((kk <= qq).astype(np.float32), (1, 4))
    cb[:, MP0:MP0 + 512] = np.tile((kk > qq).astype(np.float32), (1, 4))
    cb[:, MPO0:MPO0 + 256] = np.tile((kk > qq).astype(np.float32), (1, 2))
    cb[:, MPO0 + 256:MPO0 + 512] = np.tile((kk <= qq).astype(np.float32), (1, 2))
    return cf, cb.astype(ml_dtypes.bfloat16)


class Op:
    __slots__ = ("eng", "fn", "deps", "kind", "sem", "val", "needs_inc", "idx")


class Sched:
    def __init__(self):
        self.q = {e: [] for e in ("pe", "act", "dve", "pool", "sp")}
        self.lw = {}
        self.rd = {}
        self.dmacnt = {}

    def add(self, eng, fn, reads=(), writes=(), dma_sem=None):
        op = Op()
        op.eng = eng
        op.fn = fn
        op.deps = []
        op.kind = "dma" if dma_sem is not None else "c"
        op.needs_inc = False
        op.sem = None
        op.val = 0
        if dma_sem is not None:
            self.dmacnt[dma_sem] = self.dmacnt.get(dma_sem, 0) + 16
            op.sem = dma_sem
            op.val = self.dmacnt[dma_sem]
        deps = []
        for k in reads:
            w = self.lw.get(k)
            if w is not None:
                deps.append(w)
        for k in writes:
            w = self.lw.get(k)
            if w is not None:
                deps.append(w)
            r = self.rd.get(k)
            if r:
                deps.extend(r.values())
        seen = set()
        for d in deps:
            if d is op or id(d) in seen:
                continue
            seen.add(id(d))
            if d.kind == "c" and d.eng == eng and eng == "pe":
                continue
            op.deps.append(d)
            if d.kind == "c":
                d.needs_inc = True
        for k in reads:
            r = self.rd.setdefault(k, {})
            r[eng if op.kind == "c" else ("dma", id(op))] = op
        for k in writes:
            self.lw[k] = op
            self.rd[k] = {}
        op.idx = len(self.q[eng])
        self.q[eng].append(op)
        return op

    def finalize(self, engsem):
        for e, ops in self.q.items():
            c = 0
            for op in ops:
                if op.kind == "c" and op.needs_inc:
                    c += 1
                    op.sem = engsem[e]
                    op.val = c

    def run(self, eng, handle, semh):
        waited = {}
        for op in self.q[eng]:
            need = {}
            for d in op.deps:
                if need.get(d.sem, 0) < d.val:
                    need[d.sem] = d.val
            for sname, v in need.items():
                if waited.get(sname, 0) < v:
                    handle.wait_ge(semh[sname], v)
                    waited[sname] = v
            ins = op.fn(handle)
            if ins is None:
                continue
            if op.kind == "dma":
                ins.then_inc(semh[op.sem], 16)
            elif op.needs_inc:
                ins.then_inc(semh[op.sem], 1)


class Prog:
    def __init__(self, nc):
        self.nc = nc
        self.s = Sched()
        self.semnames = []

    def newsem(self, name):
        self.semnames.append(name)
        return name

    def mm(self, out, lhsT, rhs, start, stop, r, w):
        self.s.add("pe", lambda e: e.matmul(out, lhsT, rhs, start=start, stop=stop), r, w)

    def tr(self, out, in_, r, w):
        ident = self.IDENT
        self.s.add("pe", lambda e: e.transpose(out, in_, ident), list(r) + ["CBF"], w)

    def act(self, out, in_, func, r, w, scale=None, bias=None):
        kw = {}
        if scale is not None:
            kw["scale"] = scale
        if bias is not None:
            kw["bias"] = bias
        self.s.add("act", lambda e: e.activation(out=out, in_=in_, func=func, **kw), r, w)

    def tt(self, eng, out, in0, in1, op, r, w):
        self.s.add(eng, lambda e: e.tensor_tensor(out=out, in0=in0, in1=in1, op=op), r, w)

    def ts(self, eng, out, in0, s1, op0, r, w, s2=None, op1=None):
        if op1 is None:
            self.s.add(eng, lambda e: e.tensor_scalar(out=out, in0=in0, scalar1=s1, scalar2=None, op0=op0), r, w)
        else:
            self.s.add(eng, lambda e: e.tensor_scalar(out=out, in0=in0, scalar1=s1, scalar2=s2, op0=op0, op1=op1), r, w)

    def stt(self, out, in0, scalar, in1, op0, op1, r, w):
        self.s.add("dve", lambda e: e.scalar_tensor_tensor(out=out, in0=in0, scalar=scalar, in1=in1, op0=op0, op1=op1), r, w)

    def cp(self, eng, out, in_, r, w):
        if eng == "act":
            self.s.add("act", lambda e: e.activation(out=out, in_=in_, func=AF.Copy), r, w)
        else:
            self.s.add(eng, lambda e: e.tensor_copy(out=out, in_=in_), r, w)

    def recip(self, out, in_, r, w):
        self.s.add("dve", lambda e: e.reciprocal(out=out, in_=in_), r, w)

    def memset(self, eng, ap, val, w):
        self.s.add(eng, lambda e: e.memset(ap, val), (), w)

    def dma(self, eng, out, in_, sem, r, w):
        return self.s.add(eng, lambda e: e.dma_start(out=out, in_=in_), r, w, dma_sem=sem)

    def build(self):
        nc = self.nc
        xT = nc.dram_tensor("xT", [DM, SEQ], F32, kind="ExternalInput").ap()
        wsrc = nc.dram_tensor("wsrc", [128, TOT], F32, kind="ExternalInput").ap()
        cfd = nc.dram_tensor("cf", [128, CFW], F32, kind="ExternalInput").ap()
        cbd = nc.dram_tensor("cb", [128, CBW], BF16, kind="ExternalInput").ap()
        posd = nc.dram_tensor("pos", [128, 32], I32, kind="ExternalInput").ap()
        outT = nc.dram_tensor("outT", [DM, SEQ], F32, kind="ExternalOutput").ap()
        wb = nc.dram_tensor("wb", [128, TOT], BF16, kind="Internal").ap()
        self.wb = wb
        xTv = xT.rearrange("(k p) t -> p k t", p=128)
        oTv = outT.rearrange("(k p) t -> p k t", p=128)

        A = nc.alloc_sbuf_tensor
        CF = A("CF", [128, CFW], F32)
        CB = A("CB", [128, CBW], BF16)
        POS = A("POS", [128, 32], I32)
        COS = A("COS", [128, 1024], F32)
        SIN = A("SIN", [128, 1024], F32)
        ESK = A("ESK", [128, 16], F32)
        XX = [A(f"X{i}", [128, KD * T], F32) for i in range(2)]
        Hh = A("H", [128, KD * T], BF16)
        Ft = A("F", [128, KD * T], F32)
        SQ = A("SQ", [128, 2 * T], BF16)
        RSTD = A("RSTD", [128, T], F32)
        RTMP = A("RTMP", [128, T], F32)
        MK = A("MK", [128, 2], F32)
        UH = A("UH", [128, 4 * NFP * 2 * 2], F32)
        HPH = A("HPH", [128, 2 * KD * 16], F32)
        KK = A("KK", [128, 8 * 4 * 128], BF16)
        VR = A("VR", [128, 8 * 4 * 2 * 128], BF16)
        SL = [A(f"SL{i}", [128, SLABW], BF16) for i in range(NSLOT)]
        SH = A("SH", [128, 11520], F32)
        PS = [nc.alloc_psum_tensor(f"PS{i}", [128, 512], F32) for i in range(8)]

        self.CF, self.CB = CF, CB
        self.IDENT = CB[:, ID0:ID0 + 128]
        ONES = CB[:, ON0:ON0 + 128]
        ONEAB = [CB[:, OA0:OA0 + 128], CB[:, OB0:OB0 + 128]]
        MASK = {"own": CB[:, MO0:MO0 + 512], "prev": CB[:, MP0:MP0 + 512]}
        MASKPO = CB[:, MPO0:MPO0 + 512]
        Xv = [x[:].rearrange("p (k t) -> p k t", k=KD) for x in XX]
        Hv = Hh[:].rearrange("p (k t) -> p k t", k=KD)
        Fv = Ft[:].rearrange("p (k t) -> p k t", k=KD)
        COSv = COS[:].rearrange("p (b j) -> p b j", j=32)
        SINv = SIN[:].rearrange("p (b j) -> p b j", j=32)
        KKv = KK[:].rearrange("p (s g c) -> p s g c", s=8, g=4)
        VRv = VR[:].rearrange("p (s g h c) -> p s g h c", s=8, g=4, h=2)
        UHv = UH[:].rearrange("p (l j h c) -> p l j h c", l=4, j=NFP, h=2)
        HPHv = HPH[:].rearrange("p (l k c) -> p l k c", l=2, k=KD)
        PSB = [p[:].bitcast(BF16) for p in PS]

        def shf(a, b):
            return SH[:, a:b]

        def shb(a, b):
            return SH[:, a:b].bitcast(BF16)
        Gv = shb(0, 5632).rearrange("p (j t) -> p j t", j=NFP)
        Ut = [[shf(5632 + (2 * h + i) * 520, 5632 + (2 * h + i) * 520 + 514) for i in range(2)] for h in range(2)]
        At = [[shf(7712 + (2 * h + i) * 512, 7712 + (2 * h + i + 1) * 512) for i in range(2)] for h in range(2)]
        GE = [shf(9760 + i * 512, 9760 + (i + 1) * 512) for i in range(2)]
        HPv = shf(0, 4224).rearrange("p (k t) -> p k t", k=KD)
        Dv = shb(4224, 6272).rearrange("p (k t) -> p k t", k=KD)
        PT = [shf(6272 + i * 528, 6272 + (i + 1) * 528) for i in range(4)]
        HKVv = shb(0, 2048).rearrange("p (k t) -> p k t", k=KD)
        QTv = shb(2048, 4096).rearrange("p (m t) -> p m t", m=8)
        OTv = shb(4096, 6144).rearrange("p (m t) -> p m t", m=8)
        QR = [shb(6144 + i * 512, 6144 + (i + 1) * 512) for i in range(2)]
        RT = [[shf(7168 + (4 * i + q) * 256, 7168 + (4 * i + q + 1) * 256) for q in range(4)] for i in range(2)]
        Pt = [[shb(9216 + (2 * i + kb) * 256, 9216 + (2 * i + kb + 1) * 256) for kb in range(2)] for i in range(2)]
        Rt = [shf(10240 + i * 512, 10240 + (i + 1) * 512) for i in range(2)]
        KR = shb(11264, 11520)

        def gcol(idx, k):
            c = GV0 + idx * 8 + k
            return CF[:, c:c + 1]

        def cwcol(l, q, half, j):
            c = CW0 + ((l * 4 + q) * 2 + half) * NFP + j
            return CF[:, c:c + 1]

        sem_c = [self.newsem(f"s_const{i}") for i in range(3)]
        sem_x = [self.newsem(f"s_x{i}") for i in range(2)]
        sem_o = [self.newsem(f"s_o{i}") for i in range(2)]
        sem_sl = [self.newsem(f"s_sl{i}") for i in range(NSLOT)]
        sem_cv = [self.newsem(f"s_cv{i}") for i in range(len(CRANGES))]

        self.dma("sp", CF[:], cfd, sem_c[0], [], ["CF"])
        self.dma("sp", CB[:], cbd, sem_c[1], [], ["CBF"])
        self.dma("sp", POS[:], posd, sem_c[2], [], ["POS"])
        xload_ops = {}

        def xload(c):
            par = c % 2
            self.dma("pool", Xv[par], xTv[:, :, c * T:(c + 1) * T], sem_x[par], [],
                     [f"X{par}_{k}" for k in range(KD)])
        xload(0)
        for i, (a, b) in enumerate(CRANGES):
            self.dma("pool", wb[:, a:b], wsrc[:, a:b], sem_cv[i], [], [f"WB{i}"])
        self.memset("pool", VR[:], 0.0, [f"V{s}" for s in range(8)])
        self.memset("pool", UH[:], 0.0, [f"UH{l}_{j}_{h}" for l in range(4) for j in range(NFP) for h in range(2)])
        self.memset("pool", HPH[:], 0.0, ["HPH0", "HPH1"])
        POSF = shf(2048, 2080)
        ANG = shf(0, 1024).rearrange("p (b j) -> p b j", j=32)
        TMP = shf(1024, 2048)
        self.cp("dve", POSF, POS[:], ["POS"], ["S_POSF"])
        self.tt("dve", ANG, POSF.unsqueeze(2).to_broadcast([128, 32, 32]),
                CF[:, IF0:IF0 + 32].unsqueeze(1).to_broadcast([128, 32, 32]), ALU.mult,
                ["S_POSF", "CF"], ["S_ANG"])
        two_pi = 2.0 * math.pi
        C1 = 6.28125
        C2 = two_pi - C1
        ANGf = shf(0, 1024)
        TI = shf(2080, 3104).bitcast(I32)
        TF = shf(3104, 4128)
        for name, dst, shift in (("SIN", SIN, math.pi), ("COS", COS, 1.5 * math.pi)):
            self.ts("dve", TMP, ANGf, 1.0 / two_pi, ALU.mult, ["S_ANG"], ["S_TMP"], s2=shift / two_pi, op1=ALU.add)
            self.cp("dve", TI, TMP, ["S_TMP"], ["S_TI"])
            self.cp("dve", TF, TI, ["S_TI"], ["S_TF"])
            self.stt(TMP, TF, -C1, ANGf, ALU.mult, ALU.add, ["S_TF", "S_ANG", "S_TMP"], ["S_TMP"])
            self.stt(TMP, TF, -C2, TMP, ALU.mult, ALU.add, ["S_TF", "S_TMP"], ["S_TMP"])
            self.ts("dve", TMP, TMP, shift, ALU.add, ["S_TMP"], ["S_TMP"])
            self.ts("dve", TF, TMP, 0.0, ALU.is_lt, ["S_TMP", "S_TF"], ["S_TF"], s2=two_pi, op1=ALU.mult)
            self.tt("dve", TMP, TMP, TF, ALU.add, ["S_TMP", "S_TF"], ["S_TMP"])
            self.ts("dve", TF, TMP, two_pi, ALU.is_ge, ["S_TMP", "S_TF"], ["S_TF"], s2=-two_pi, op1=ALU.mult)
            self.tt("dve", TMP, TMP, TF, ALU.add, ["S_TMP", "S_TF"], ["S_TMP"])
            self.ts("dve", TMP, TMP, -math.pi, ALU.add, ["S_TMP"], ["S_TMP"])
            self.ts("dve", TMP, TMP, math.pi, ALU.min, ["S_TMP"], ["S_TMP"], s2=-math.pi, op1=ALU.max)
            self.act(dst[:], TMP, AF.Sin, ["S_TMP"], [name])
        self.act(ESK[:], CF[:, SK0:SK0 + 16], AF.Exp, ["CF"], ["ESK"])
        self.cp("dve", MK[:, 0:1], COS[:, 0:1], ["COS", "SIN", "ESK"], ["CHUNKDONE"])

        self.slab_n = 0

        def slab(key):
            o, n = OFFS[key]
            i = self.slab_n % NSLOT
            self.slab_n += 1
            self.dma("sp", SL[i][:, 0:n], wb[:, o:o + n], sem_sl[i], [f"WB{range_of(key)}"], [f"SL{i}"])
            return SL[i], f"SL{i}"

        def norm_stats(src_view, src_keys, bank):
            for k in range(KD):
                sq = SQ[:, (k % 2) * T:(k % 2 + 1) * T]
                self.act(sq, src_view[:, k, :], AF.Square, [src_keys[k]], [f"SQ{k % 2}"])
                self.mm(PS[bank][:], ONES, sq, k == 0, k == KD - 1, [f"SQ{k % 2}", "CBF"], [f"PS{bank}"])
            rstd_from(bank)

        def rstd_from(bank):
            self.act(RTMP[:], PS[bank][:], AF.Sqrt, [f"PS{bank}", "CF"], ["RTMP"], scale=1.0 / DM,
                     bias=CF[:, EP0:EP0 + 1])
            self.recip(RSTD[:], RTMP[:], ["RTMP"], ["RSTD"])

        def norm_apply(dst_view, dst_keys, par, gidx, extra_r=()):
            for k in range(KD):
                self.stt(dst_view(k), Xv[par][:, k, :], gcol(gidx, k), RSTD[:], ALU.mult, ALU.mult,
                         [f"X{par}_{k}", "RSTD", "CF"] + list(extra_r), [dst_keys[k]])

        def post_norm_residual(par, gidx, bank):
            for k in range(KD):
                self.stt(Fv[:, k, :], Fv[:, k, :], gcol(gidx, k), RSTD[:], ALU.mult, ALU.mult,
                         [f"F{k}", "RSTD", "CF"], [f"F{k}"])
                self.tt("pool", Xv[par][:, k, :], Xv[par][:, k, :], Fv[:, k, :], ALU.add,
                        [f"F{k}", f"X{par}_{k}"], [f"X{par}_{k}"])

        def pool_mixer(c, l, par):
            norm_stats(Xv[par], [f"X{par}_{k}" for k in range(KD)], 6)
            norm_apply(lambda k: HPv[:, k, 16:528], [f"HP{k}" for k in range(KD)], par, l, extra_r=["CHUNKDONE"])
            hpk = [f"HP{k}" for k in range(KD)]
            self.cp("pool", HPv[:, :, 0:16], HPHv[:, l, :, :], [f"HPH{l}"] + hpk, hpk)
            for k in range(KD):
                gi = k // 2
                eng = "dve" if k % 2 == 0 else "pool"
                t1, t2 = (PT[0], PT[1]) if eng == "dve" else (PT[2], PT[3])
                tk = ("PT0", "PT1") if eng == "dve" else ("PT2", "PT3")
                src, srck = HPv[:, k, :], f"HP{k}"
                lo = 0
                dst, dstk = t1, tk[0]
                for step in range(gi + 1):
                    sh = 1 << step
                    nlo = lo + sh
                    self.tt(eng, dst[:, nlo:528], src[:, nlo:528], src[:, lo:528 - sh], ALU.add, [srck], [dstk])
                    src, srck = dst, dstk
                    lo = nlo
                    dst, dstk = (t2, tk[1]) if dst is t1 else (t1, tk[0])
                w = POOL_W[gi]
                self.stt(Dv[:, k, :], src[:, 16:528], 1.0 / w, HPv[:, k, 16:528], ALU.mult, ALU.subtract,
                         [srck, f"HP{k}"], [f"Dd{k}"])
                if c == 0:
                    ic = CF[:, IC0 + gi * 16:IC0 + gi * 16 + 16]
                    self.tt("dve", src[:, 16:32], src[:, 16:32], ic, ALU.mult, [srck, "CF"], [srck])
                    self.tt("dve", Dv[:, k, 0:16], src[:, 16:32], HPv[:, k, 16:32], ALU.subtract,
                            [srck, f"HP{k}"], [f"Dd{k}"])
            self.cp("pool", HPHv[:, l, :, :], HPv[:, :, 512:528], hpk, [f"HPH{l}"])
            sl, slk = slab(f"pool{l}")
            wv = sl[:, 0:2048].rearrange("p (g kk c) -> p g kk c", g=4, kk=2)
            for ko in range(KD):
                gi, mm_ = ko // 2, ko % 2
                bank = ko % 4
                for kk in range(2):
                    self.mm(PS[bank][:], wv[:, gi, kk, mm_ * 128:(mm_ + 1) * 128], Dv[:, 2 * gi + kk, :],
                            kk == 0, kk == 1, [slk, f"Dd{2 * gi + kk}"], [f"PS{bank}"])
                self.act(Fv[:, ko, :], PS[bank][:], AF.Identity, [f"PS{bank}", "CF"], [f"F{ko}"],
                         scale=gcol(17 + l, ko))
                sq = SQ[:, (ko % 2) * T:(ko % 2 + 1) * T]
                self.act(sq, PS[bank][:], AF.Square, [f"PS{bank}", "CF"], [f"SQ{ko % 2}"], scale=gcol(17 + l, ko))
                self.mm(PS[6][:], ONES, sq, ko == 0, ko == KD - 1, [f"SQ{ko % 2}", "CBF"], ["PS6"])
            rstd_from(6)
            post_norm_residual(par, 4 + l, 6)

        def ffn(c, l, par):
            norm_stats(Xv[par], [f"X{par}_{k}" for k in range(KD)], 6)
            norm_apply(lambda k: Hv[:, k, :], [f"H{k}" for k in range(KD)], par, 8 + l)
            for s in range(11):
                sl, slk = slab(f"win{l}_{s}")
                wv = sl[:].rearrange("p (fp h k c) -> p fp h k c", fp=2, h=2, k=KD)
                for fp in range(2):
                    j = 2 * s + fp
                    jb = j % 2
                    for half in range(2):
                        bank = 2 * jb + half
                        for k in range(KD):
                            self.mm(PS[bank][:], wv[:, fp, half, k, :], Hv[:, k, :], k == 0, k == KD - 1,
                                    [slk, f"H{k}"], [f"PS{bank}"])
                    for half in range(2):
                        bank = 2 * jb + half
                        U, uk = Ut[half][jb], f"U{half}{jb}"
                        Aa, ak = At[half][jb], f"A{half}{jb}"
                        hk = f"UH{l}_{j}_{half}"
                        self.cp("act", U[:, 2:514], PS[bank][:], [f"PS{bank}"], [uk])
                        self.cp("pool", U[:, 0:2], UHv[:, l, j, half, :], [hk, uk], [uk])
                        self.act(Aa, PS[bank][:], AF.Identity, [f"PS{bank}", "CF"], [ak],
                                 scale=cwcol(l, 2, half, j), bias=cwcol(l, 3, half, j))
                    for q, off in ((1, 1), (0, 0)):
                        for half in range(2):
                            U, uk = Ut[half][jb], f"U{half}{jb}"
                            Aa, ak = At[half][jb], f"A{half}{jb}"
                            self.stt(Aa, U[:, off:off + 512], cwcol(l, q, half, j), Aa, ALU.mult, ALU.add,
                                     [uk, ak, "CF"], [ak])
                    for half in range(2):
                        U, uk = Ut[half][jb], f"U{half}{jb}"
                        self.cp("pool", UHv[:, l, j, half, :], U[:, 512:514], [uk], [f"UH{l}_{j}_{half}"])
                    self.act(GE[jb], At[0][jb], AF.Gelu_apprx_tanh, [f"A0{jb}"], [f"GE{jb}"])
                    self.tt("pool", Gv[:, j, :], GE[jb], At[1][jb], ALU.mult, [f"GE{jb}", f"A1{jb}"], [f"G{j}"])
            pend = None
            for m in range(8):
                sl, slk = slab(f"wout{l}_{m}")
                wv = sl[:, 0:FF].rearrange("p (j c) -> p j c", j=NFP)
                bank = 4 + m % 2
                for j in range(NFP):
                    self.mm(PS[bank][:], wv[:, j, :], Gv[:, j, :], j == 0, j == NFP - 1, [slk, f"G{j}"], [f"PS{bank}"])
                if pend is not None:
                    self.mm(PS[6][:], ONES, pend[0], pend[1] == 0, False, [pend[2], "CBF"], ["PS6"])
                self.cp("act", Fv[:, m, :], PS[bank][:], [f"PS{bank}"], [f"F{m}"])
                sq = SQ[:, (m % 2) * T:(m % 2 + 1) * T]
                self.act(sq, PS[bank][:], AF.Square, [f"PS{bank}"], [f"SQ{m % 2}"])
                pend = (sq, m, f"SQ{m % 2}")
            self.mm(PS[6][:], ONES, pend[0], False, True, [pend[2], "CBF"], ["PS6"])
            rstd_from(6)
            post_norm_residual(par, 12 + l, 6)

        def rope(psv, pskey, nh, blk, outv, outkey, rset):
            cb = COSv[:, blk, :].unsqueeze(1).to_broadcast([128, nh, 32])
            sb = SINv[:, blk, :].unsqueeze(1).to_broadcast([128, nh, 32])
            x1, x2 = psv[:, :, 0:32], psv[:, :, 32:64]
            r = [RT[rset][q][:, 0:nh * 32].rearrange("p (h j) -> p h j", j=32) for q in range(4)]
            rk = [f"RT{rset}{q}" for q in range(4)]
            self.tt("dve", r[0], x1, cb, ALU.mult, [pskey, "COS"], [rk[0]])
            self.tt("dve", r[1], x2, sb, ALU.mult, [pskey, "SIN"], [rk[1]])
            self.tt("dve", r[2], x2, cb, ALU.mult, [pskey, "COS"], [rk[2]])
            self.tt("dve", r[3], x1, sb, ALU.mult, [pskey, "SIN"], [rk[3]])
            self.tt("pool", outv[:, :, 0:32], r[0], r[1], ALU.subtract, [rk[0], rk[1]], [outkey])
            self.tt("pool", outv[:, :, 32:64], r[2], r[3], ALU.add, [rk[2], rk[3]], [outkey])

        def kv_step(c, par):
            norm_stats(Xv[par], [f"X{par}_{k}" for k in range(KD)], 6)
            norm_apply(lambda k: Hv[:, k, :], [f"H{k}" for k in range(KD)], par, 2)
            norm_apply(lambda k: HKVv[:, k, :], [f"HKV{k}" for k in range(KD)], par, 16)
            slk_, slkk = slab("kvk")
            slv_, slvk = slab("kvv")
            wk = slk_[:].rearrange("p (k c) -> p k c", k=KD)
            wvv = slv_[:, 0:2048].rearrange("p (k c) -> p k c", k=KD)
            for b in range(4):
                slot = 4 * (c % 2) + b
                bk, bv = 2 * (b % 2), 2 * (b % 2) + 1
                for k in range(KD):
                    self.mm(PS[bk][:], HKVv[:, k, b * 128:(b + 1) * 128], wk[:, k, :], k == 0, k == KD - 1,
                            [slkk, f"HKV{k}"], [f"PS{bk}"])
                for k in range(KD):
                    self.mm(PS[bv][:, 0:256], HKVv[:, k, b * 128:(b + 1) * 128], wvv[:, k, :], k == 0, k == KD - 1,
                            [slvk, f"HKV{k}"], [f"PS{bv}"])
                rope(PS[bk][:].rearrange("p (h d) -> p h d", d=64), f"PS{bk}", 8, 4 * c + b,
                     KR.rearrange("p (h d) -> p h d", d=64), "KR", b % 2)
                for g in range(4):
                    self.tr(PSB[4][:, g * 128:(g + 1) * 128], KR[:, g * 128:(g + 1) * 128], ["KR"], ["PS4"])
                self.cp("act", KKv[:, slot, :, :], PSB[4][:, 0:512].rearrange("p (g c) -> p g c", g=4),
                        ["PS4"], [f"KK{slot}"])
                vps = PS[bv][:, 0:256].rearrange("p (g d) -> p g d", g=4)
                self.cp("act", VRv[:, slot, :, 0, 0:64], vps, [f"PS{bv}"], [f"V{slot}"])
                self.cp("dve", VRv[:, slot, :, 1, 64:128], vps, [f"PS{bv}"], [f"V{slot}"])

        def attn_mixer(c, l, par):
            j = l - 2
            if l == 3:
                norm_stats(Xv[par], [f"X{par}_{k}" for k in range(KD)], 6)
                norm_apply(lambda k: Hv[:, k, :], [f"H{k}" for k in range(KD)], par, l)
            wq = []
            for hf in range(2):
                sl, slk = slab(f"wq{l}_{hf}")
                wq.append((sl[:].rearrange("p (k c) -> p k c", k=KD), slk))
            for b in range(4):
                qr = QR[b % 2]
                for hf in range(2):
                    bank = 2 * (b % 2) + hf
                    for k in range(KD):
                        self.mm(PS[bank][:], Hv[:, k, b * 128:(b + 1) * 128], wq[hf][0][:, k, :], k == 0, k == KD - 1,
                                [wq[hf][1], f"H{k}"], [f"PS{bank}"])
                    rope(PS[bank][:].rearrange("p (h d) -> p h d", d=64), f"PS{bank}", 8, 4 * c + b,
                         qr[:, hf * 512:(hf + 1) * 512].rearrange("p (h d) -> p h d", d=64), f"QR{b % 2}", hf)
                for m in range(8):
                    self.tr(PSB[4][:, m * 128:(m + 1) * 128], qr[:, m * 128:(m + 1) * 128], [f"QR{b % 2}"], ["PS4"])
                self.cp("act", QTv[:, :, b * 128:(b + 1) * 128], PSB[4][:].rearrange("p (m t) -> p m t", m=8),
                        ["PS4"], [f"QT{m}" for m in range(8)])
            if DBG_LEVEL <= 2:
                return
            steps = [(g, b) for g in range(4) for b in range(4)]

            def scores(i):
                g, b = steps[i]
                blk = 4 * c + b
                kbs = ["prev", "own"] if blk > 0 else ["own"]
                pb = i % 2
                nk = len(kbs)
                for kbi, kb in enumerate(kbs):
                    slot = (4 * (c % 2) + b - (1 if kb == "prev" else 0)) % 8
                    for ii in range(4):
                        h = 4 * g + ii
                        m, r0 = h // 2, 64 * (h % 2)
                        bank = 2 * pb + (ii % 2)
                        col = (kbi * 2 + ii // 2) * 128
                        self.mm(PS[bank][:, col:col + 128], KKv[r0:r0 + 64, slot, g, :],
                                QTv[r0:r0 + 64, m, b * 128:(b + 1) * 128], True, True,
                                [f"KK{slot}", f"QT{m}"], [f"PS{bank}"])
                for par_ in range(2):
                    bank = 2 * pb + par_
                    P, pk = Pt[pb][par_], f"P{pb}{par_}"
                    self.act(P[:, 0:nk * 256], PS[bank][:, 0:nk * 256], AF.Exp, [f"PS{bank}"], [pk], scale=0.125)
                    if nk == 2:
                        self.tt("pool", P, P, MASKPO, ALU.mult, [pk, "CBF"], [pk])
                    else:
                        self.tt("pool", P[:, 0:256], P[:, 0:256], MASK["own"][:, 0:256], ALU.mult, [pk, "CBF"], [pk])
                return kbs

            def pv(i, kbs):
                g, b = steps[i]
                pb = i % 2
                for pm in range(2):
                    for which, bank0 in (("V", 4), ("D", 6)):
                        bank = bank0 + pm
                        seq = [(hh, kbi) for hh in range(2) for kbi in range(len(kbs))]
                        for idx, (hh, kbi) in enumerate(seq):
                            slot = (4 * (c % 2) + b - (1 if kbs[kbi] == "prev" else 0)) % 8
                            lhsT = VRv[:, slot, g, hh, :] if which == "V" else ONEAB[hh]
                            rk = [f"V{slot}"] if which == "V" else ["CBF"]
                            col = (kbi * 2 + pm) * 128
                            self.mm(PS[bank][:, b * 128:(b + 1) * 128], lhsT,
                                    Pt[pb][hh][:, col:col + 128], idx == 0, idx == len(seq) - 1,
                                    rk + [f"P{pb}{hh}"], [f"PS{bank}"])
                if b == 3:
                    for pm in range(2):
                        m = 2 * g + pm
                        R, rk = Rt[pm], f"R{pm}"
                        self.ts("dve", R, PS[6 + pm][:], ESK[:, j * 8 + m:j * 8 + m + 1], ALU.add,
                                [f"PS{6 + pm}", "ESK"], [rk])
                        self.recip(R, R, [rk], [rk])
                        self.tt("dve", OTv[:, m, :], PS[4 + pm][:], R, ALU.mult, [f"PS{4 + pm}", rk], [f"OT{m}"])

            prev = None
            for i in range(len(steps)):
                kbs = scores(i)
                if prev is not None:
                    pv(i - 1, prev)
                prev = kbs
            pv(len(steps) - 1, prev)
            if DBG_LEVEL <= 3:
                return
            wo = []
            for hf in range(2):
                sl, slk = slab(f"wo{l}_{hf}")
                wo.append((sl[:].rearrange("p (m c) -> p m c", m=8), slk))
            pend = None
            for mo in range(8):
                hf = mo // 4
                bank = mo % 4
                for m in range(8):
                    self.mm(PS[bank][:], wo[hf][0][:, m, (mo % 4) * 128:(mo % 4 + 1) * 128], OTv[:, m, :],
                            m == 0, m == 7, [wo[hf][1], f"OT{m}"], [f"PS{bank}"])
                if pend is not None:
                    self.mm(PS[6][:], ONES, pend[0], pend[1] == 0, False, [pend[2], "CBF"], ["PS6"])
                self.cp("act", Fv[:, mo, :], PS[bank][:], [f"PS{bank}"], [f"F{mo}"])
                sq = SQ[:, (mo % 2) * T:(mo % 2 + 1) * T]
                self.act(sq, PS[bank][:], AF.Square, [f"PS{bank}"], [f"SQ{mo % 2}"])
                pend = (sq, mo, f"SQ{mo % 2}")
            self.mm(PS[6][:], ONES, pend[0], False, True, [pend[2], "CBF"], ["PS6"])
            rstd_from(6)
            post_norm_residual(par, 4 + l, 6)

        store_ops = []
        for c in range(NCH_RUN):
            par = c % 2
            for l in range(NL_RUN):
                if l < 2:
                    pool_mixer(c, l, par)
                else:
                    if l == 2:
                        kv_step(c, par)
                    if DBG_LEVEL <= 1:
                        break
                    attn_mixer(c, l, par)
                    if DBG_LEVEL <= 3:
                        break
                if l == 0 and c + 1 < NCH_RUN:
                    xload(c + 1)
                ffn(c, l, par)
            xk = [f"X{par}_{k}" for k in range(KD)]
            self.cp("dve", MK[:, 0:1], Xv[par][:, 0, 0:1], xk, ["CHUNKDONE"])
            store_ops.append(self.dma("pool", oTv[:, :, c * T:(c + 1) * T], Xv[par], sem_o[par], xk, []))
        fin = self.s.add("pool", lambda e: None, [], [])
        fin.deps = store_ops[-2:] if len(store_ops) >= 2 else store_ops[-1:]

    def emit(self):
        nc = self.nc
        s = self.s
        engsem = {e: f"s_{e}" for e in ("pe", "act", "dve", "pool", "sp")}
        s.finalize(engsem)
        with ExitStack() as es:
            semh = {}
            for n in list(engsem.values()) + self.semnames:
                semh[n] = es.enter_context(nc.semaphore(n))
            block = es.enter_context(nc.Block())

            @block.tensor
            def _(e):
                s.run("pe", e, semh)

            @block.scalar
            def _(e):
                s.run("act", e, semh)

            @block.vector
            def _(e):
                s.run("dve", e, semh)

            @block.gpsimd
            def _(e):
                s.run("pool", e, semh)

            @block.sync
            def _(e):
                s.run("sp", e, semh)


def build_nc():
    nc = bass.Bass("TRN2", target_bir_lowering=False)
    p = Prog(nc)
    p.build()
    p.emit()
    return nc


def kernel(**inputs):
    x = np.asarray(inputs["x"], np.float32)
    pos = np.asarray(inputs["positions"]).astype(np.int32)
    W = pack_weights(inputs)
    cf, cb = pack_consts(inputs)
    nc = build_nc()
    in_maps = []
    for b in range(NCORE):
        in_maps.append({
            "xT": np.ascontiguousarray(x[b].T),
            "wsrc": W,
            "cf": cf,
            "cb": cb,
            "pos": np.ascontiguousarray(pos[b].reshape(32, 128).T),
        })
    res = run_bass_kernel_spmd(nc, in_maps, core_ids=list(range(NCORE)))
    out = np.stack([np.asarray(r["outT"]).T for r in res.results], axis=0)
    return np.ascontiguousarray(out.astype(np.float32))
```

```python
import math
from contextlib import ExitStack

import numpy as np
import ml_dtypes

import concourse.bass as bass
import concourse.mybir as mybir
from concourse.bass_utils import run_bass_kernel_spmd

F32 = mybir.dt.float32
BF16 = mybir.dt.bfloat16
I32 = mybir.dt.int32
AF = mybir.ActivationFunctionType
ALU = mybir.AluOpType

DM = 1024
SEQ = 4096
NCORE = 8
T = 512
NCH = SEQ // T
KD = DM // 128
FF = 2816
NFP = FF // 128
NSLOT = 5
SLABW = 4096
EPS = 1e-6
NL_RUN = 4
NCH_RUN = NCH
DBG_LEVEL = 9

GV0 = 0
CW0 = GV0 + 19 * 8
SK0 = CW0 + 4 * 4 * 2 * NFP
IF0 = SK0 + 16
IC0 = IF0 + 32
EP0 = IC0 + 64
CFW = 976
ID0, ON0, OA0, OB0, MO0, MP0, MPO0 = 0, 128, 256, 384, 512, 1024, 1536
CBW = 2048
POOL_W = (2, 4, 8, 16)


def slab_plan():
    plan = []
    for l in range(4):
        if l < 2:
            plan.append((f"pool{l}", 2048))
        else:
            if l == 2:
                plan += [("kvk", 4096), ("kvv", 2048)]
            plan += [(f"wq{l}_0", 4096), (f"wq{l}_1", 4096), (f"wo{l}_0", 4096), (f"wo{l}_1", 4096)]
        plan += [(f"win{l}_{s}", 4096) for s in range(11)]
        plan += [(f"wout{l}_{m}", 2816) for m in range(8)]
    offs = {}
    o = 0
    for k, n in plan:
        offs[k] = (o, n)
        o += n
    return plan, offs, o


PLAN, OFFS, TOT = slab_plan()


def conv_ranges():
    rngs = []
    cur0, cur1 = 0, 0
    for k, n in PLAN:
        o, _ = OFFS[k]
        if cur1 - cur0 + n > 14336:
            rngs.append((cur0, cur1))
            cur0 = cur1
        cur1 = o + n
    rngs.append((cur0, cur1))
    return rngs


CRANGES = conv_ranges()


def range_of(key):
    o, n = OFFS[key]
    for i, (a, b) in enumerate(CRANGES):
        if a <= o and o + n <= b:
            return i
    raise AssertionError(key)


def pack_weights(inp):
    W = np.zeros((128, TOT), np.float32)

    def put(key, arr):
        o, n = OFFS[key]
        W[:, o:o + n] = np.ascontiguousarray(arr).reshape(128, n)

    for l in range(4):
        if l < 2:
            pw = np.asarray(inp["pool_w"][l])
            put(f"pool{l}", pw.reshape(4, 2, 128, 256).transpose(2, 0, 1, 3))
        else:
            j = l - 2
            if l == 2:
                wkv = np.asarray(inp["w_kv"])
                wk = wkv[:, :256].reshape(8, 128, 4, 64)
                kd = np.broadcast_to(wk[:, :, :, None, :], (8, 128, 4, 2, 64))
                put("kvk", kd.transpose(1, 0, 2, 3, 4))
                put("kvv", wkv[:, 256:].reshape(8, 128, 256).transpose(1, 0, 2))
            wq = np.asarray(inp["w_q"][j]).reshape(8, 128, 2, 512)
            wo = np.asarray(inp["w_o"][j]).reshape(8, 128, 2, 512)
            for hf in range(2):
                put(f"wq{l}_{hf}", wq[:, :, hf, :].transpose(1, 0, 2))
                put(f"wo{l}_{hf}", wo[:, :, hf, :].transpose(1, 0, 2))
        win = np.asarray(inp["ffn_w_in"][l]).reshape(8, 128, 2, NFP, 128)
        for s in range(11):
            put(f"win{l}_{s}", win[:, :, :, 2 * s:2 * s + 2, :].transpose(1, 3, 2, 0, 4))
        wout = np.asarray(inp["ffn_w_out"][l]).reshape(NFP, 128, 8, 128)
        for m in range(8):
            put(f"wout{l}_{m}", wout[:, :, m, :].transpose(1, 0, 2))
    return W


def pack_consts(inp):
    cf = np.zeros((128, CFW), np.float32)

    def col(v):
        return np.asarray(v, np.float32).reshape(8, 128).T

    vecs = [inp["mix_pre_g"][l] for l in range(4)] + [inp["mix_post_g"][l] for l in range(4)] \
        + [inp["ffn_pre_g"][l] for l in range(4)] + [inp["ffn_post_g"][l] for l in range(4)] \
        + [inp["kv_norm_g"]] + [inp["pool_scale"][l] for l in range(2)]
    for i, v in enumerate(vecs):
        cf[:, GV0 + i * 8:GV0 + i * 8 + 8] = col(v)
    cwv = np.asarray(inp["ffn_conv_w"], np.float32)
    cbv = np.asarray(inp["ffn_conv_b"], np.float32)
    for l in range(4):
        for q in range(4):
            v = cwv[l, q] if q < 3 else cbv[l]
            for half in range(2):
                c0 = CW0 + ((l * 4 + q) * 2 + half) * NFP
                cf[:, c0:c0 + NFP] = v[half * FF:(half + 1) * FF].reshape(NFP, 128).T
    sk = np.asarray(inp["sinks"], np.float32)
    for j in range(2):
        for m in range(8):
            cf[0:64, SK0 + j * 8 + m] = sk[j, 2 * m]
            cf[64:128, SK0 + j * 8 + m] = sk[j, 2 * m + 1]
    invf = (1.0 / (10000.0 ** (np.arange(0, 64, 2, dtype=np.float32) / 64.0))).astype(np.float32)
    cf[:, IF0:IF0 + 32] = invf[None, :]
    for g, w in enumerate(POOL_W):
        cf[:, IC0 + g * 16:IC0 + g * 16 + 16] = (1.0 / np.minimum(np.arange(1, 17), w)).astype(np.float32)[None, :]
    cf[:, EP0] = EPS
    cb = np.zeros((128, CBW), np.float32)
    cb[:, ID0:ID0 + 128] = np.eye(128)
    cb[:, ON0:ON0 + 128] = 1.0
    cb[:, OA0:OA0 + 64] = 1.0
    cb[:, OB0 + 64:OB0 + 128] = 1.0
    kk = np.arange(128)[:, None]
    qq = np.arange(128)[None, :]
    cb[:, MO0:MO0 + 512] = np.tile((kk <= qq).astype(np.float32), (1, 4))
    cb[:, MP0:MP0 + 512] = np.tile((kk > qq).astype(np.float32), (1, 4))
    cb[:, MPO0:MPO0 + 256] = np.tile((kk > qq).astype(np.float32), (1, 2))
    cb[:, MPO0 + 256:MPO0 + 512] = np.tile((kk <= qq).astype(np.float32), (1, 2))
    return cf, cb.astype(ml_dtypes.bfloat16)


class Op:
    __slots__ = ("eng", "fn", "deps", "kind", "sem", "val", "needs_inc", "idx")


class Sched:
    def __init__(self):
        self.q = {e: [] for e in ("pe", "act", "dve", "pool", "sp")}
        self.lw = {}
        self.rd = {}
        self.dmacnt = {}

    def add(self, eng, fn, reads=(), writes=(), dma_sem=None):
        op = Op()
        op.eng = eng
        op.fn = fn
        op.deps = []
        op.kind = "dma" if dma_sem is not None else "c"
        op.needs_inc = False
        op.sem = None
        op.val = 0
        if dma_sem is not None:
            self.dmacnt[dma_sem] = self.dmacnt.get(dma_sem, 0) + 16
            op.sem = dma_sem
            op.val = self.dmacnt[dma_sem]
        deps = []
        for k in reads:
            w = self.lw.get(k)
            if w is not None:
                deps.append(w)
        for k in writes:
            w = self.lw.get(k)
            if w is not None:
                deps.append(w)
            r = self.rd.get(k)
            if r:
                deps.extend(r.values())
        seen = set()
        for d in deps:
            if d is op or id(d) in seen:
                continue
            seen.add(id(d))
            if d.kind == "c" and d.eng == eng and eng == "pe":
                continue
            op.deps.append(d)
            if d.kind == "c":
                d.needs_inc = True
        for k in reads:
            r = self.rd.setdefault(k, {})
            r[eng if op.kind == "c" else ("dma", id(op))] = op
        for k in writes:
            self.lw[k] = op
            self.rd[k] = {}
        op.idx = len(self.q[eng])
        self.q[eng].append(op)
        return op

    def finalize(self, engsem):
        for e, ops in self.q.items():
            c = 0
            for op in ops:
                if op.kind == "c" and op.needs_inc:
                    c += 1
                    op.sem = engsem[e]
                    op.val = c

    def run(self, eng, handle, semh):
        waited = {}
        for op in self.q[eng]:
            need = {}
            for d in op.deps:
                if need.get(d.sem, 0) < d.val:
                    need[d.sem] = d.val
            for sname, v in need.items():
                if waited.get(sname, 0) < v:
                    handle.wait_ge(semh[sname], v)
                    waited[sname] = v
            ins = op.fn(handle)
            if ins is None:
                continue
            if op.kind == "dma":
                ins.then_inc(semh[op.sem], 16)
            elif op.needs_inc:
                ins.then_inc(semh[op.sem], 1)


class Prog:
    def __init__(self, nc):
        self.nc = nc
        self.s = Sched()
        self.semnames = []

    def newsem(self, name):
        self.semnames.append(name)
        return name

    def mm(self, out, lhsT, rhs, start, stop, r, w):
        self.s.add("pe", lambda e: e.matmul(out, lhsT, rhs, start=start, stop=stop), r, w)

    def tr(self, out, in_, r, w):
        ident = self.IDENT
        self.s.add("pe", lambda e: e.transpose(out, in_, ident), list(r) + ["CBF"], w)

    def act(self, out, in_, func, r, w, scale=None, bias=None):
        kw = {}
        if scale is not None:
            kw["scale"] = scale
        if bias is not None:
            kw["bias"] = bias
        self.s.add("act", lambda e: e.activation(out=out, in_=in_, func=func, **kw), r, w)

    def tt(self, eng, out, in0, in1, op, r, w):
        self.s.add(eng, lambda e: e.tensor_tensor(out=out, in0=in0, in1=in1, op=op), r, w)

    def ts(self, eng, out, in0, s1, op0, r, w, s2=None, op1=None):
        if op1 is None:
            self.s.add(eng, lambda e: e.tensor_scalar(out=out, in0=in0, scalar1=s1, scalar2=None, op0=op0), r, w)
        else:
            self.s.add(eng, lambda e: e.tensor_scalar(out=out, in0=in0, scalar1=s1, scalar2=s2, op0=op0, op1=op1), r, w)

    def stt(self, out, in0, scalar, in1, op0, op1, r, w):
        self.s.add("dve", lambda e: e.scalar_tensor_tensor(out=out, in0=in0, scalar=scalar, in1=in1, op0=op0, op1=op1), r, w)

    def cp(self, eng, out, in_, r, w):
        if eng == "act":
            self.s.add("act", lambda e: e.activation(out=out, in_=in_, func=AF.Copy), r, w)
        else:
            self.s.add(eng, lambda e: e.tensor_copy(out=out, in_=in_), r, w)

    def recip(self, out, in_, r, w):
        self.s.add("dve", lambda e: e.reciprocal(out=out, in_=in_), r, w)

    def memset(self, eng, ap, val, w):
        self.s.add(eng, lambda e: e.memset(ap, val), (), w)

    def dma(self, eng, out, in_, sem, r, w):
        return self.s.add(eng, lambda e: e.dma_start(out=out, in_=in_), r, w, dma_sem=sem)

    def build(self):
        nc = self.nc
        xT = nc.dram_tensor("xT", [DM, SEQ], F32, kind="ExternalInput").ap()
        wsrc = nc.dram_tensor("wsrc", [128, TOT], F32, kind="ExternalInput").ap()
        cfd = nc.dram_tensor("cf", [128, CFW], F32, kind="ExternalInput").ap()
        cbd = nc.dram_tensor("cb", [128, CBW], BF16, kind="ExternalInput").ap()
        posd = nc.dram_tensor("pos", [128, 32], I32, kind="ExternalInput").ap()
        outT = nc.dram_tensor("outT", [DM, SEQ], F32, kind="ExternalOutput").ap()
        wb = nc.dram_tensor("wb", [128, TOT], BF16, kind="Internal").ap()
        self.wb = wb
        xTv = xT.rearrange("(k p) t -> p k t", p=128)
        oTv = outT.rearrange("(k p) t -> p k t", p=128)

        A = nc.alloc_sbuf_tensor
        CF = A("CF", [128, CFW], F32)
        CB = A("CB", [128, CBW], BF16)
        POS = A("POS", [128, 32], I32)
        COS = A("COS", [128, 1024], F32)
        SIN = A("SIN", [128, 1024], F32)
        ESK = A("ESK", [128, 16], F32)
        XX = [A(f"X{i}", [128, KD * T], F32) for i in range(2)]
        Hh = A("H", [128, KD * T], BF16)
        Ft = A("F", [128, KD * T], F32)
        SQ = A("SQ", [128, 2 * T], BF16)
        RSTD = A("RSTD", [128, T], F32)
        RTMP = A("RTMP", [128, T], F32)
        MK = A("MK", [128, 2], F32)
        UH = A("UH", [128, 4 * NFP * 2 * 2], F32)
        HPH = A("HPH", [128, 2 * KD * 16], F32)
        KK = A("KK", [128, 8 * 4 * 128], BF16)
        VR = A("VR", [128, 8 * 4 * 2 * 128], BF16)
        SL = [A(f"SL{i}", [128, SLABW], BF16) for i in range(NSLOT)]
        SH = A("SH", [128, 11520], F32)
        PS = [nc.alloc_psum_tensor(f"PS{i}", [128, 512], F32) for i in range(8)]

        self.CF, self.CB = CF, CB
        self.IDENT = CB[:, ID0:ID0 + 128]
        ONES = CB[:, ON0:ON0 + 128]
        ONEAB = [CB[:, OA0:OA0 + 128], CB[:, OB0:OB0 + 128]]
        MASK = {"own": CB[:, MO0:MO0 + 512], "prev": CB[:, MP0:MP0 + 512]}
        MASKPO = CB[:, MPO0:MPO0 + 512]
        Xv = [x[:].rearrange("p (k t) -> p k t", k=KD) for x in XX]
        Hv = Hh[:].rearrange("p (k t) -> p k t", k=KD)
        Fv = Ft[:].rearrange("p (k t) -> p k t", k=KD)
        COSv = COS[:].rearrange("p (b j) -> p b j", j=32)
        SINv = SIN[:].rearrange("p (b j) -> p b j", j=32)
        KKv = KK[:].rearrange("p (s g c) -> p s g c", s=8, g=4)
        VRv = VR[:].rearrange("p (s g h c) -> p s g h c", s=8, g=4, h=2)
        UHv = UH[:].rearrange("p (l j h c) -> p l j h c", l=4, j=NFP, h=2)
        HPHv = HPH[:].rearrange("p (l k c) -> p l k c", l=2, k=KD)
        PSB = [p[:].bitcast(BF16) for p in PS]

        def shf(a, b):
            return SH[:, a:b]

        def shb(a, b):
            return SH[:, a:b].bitcast(BF16)
        Gv = shb(0, 5632).rearrange("p (j t) -> p j t", j=NFP)
        Ut = [[shf(5632 + (2 * h + i) * 520, 5632 + (2 * h + i) * 520 + 514) for i in range(2)] for h in range(2)]
        At = [[shf(7712 + (2 * h + i) * 512, 7712 + (2 * h + i + 1) * 512) for i in range(2)] for h in range(2)]
        GE = [shf(9760 + i * 512, 9760 + (i + 1) * 512) for i in range(2)]
        HPv = shf(0, 4224).rearrange("p (k t) -> p k t", k=KD)
        Dv = shb(4224, 6272).rearrange("p (k t) -> p k t", k=KD)
        PT = [shf(6272 + i * 528, 6272 + (i + 1) * 528) for i in range(4)]
        HKVv = shb(0, 2048).rearrange("p (k t) -> p k t", k=KD)
        QTv = shb(2048, 4096).rearrange("p (m t) -> p m t", m=8)
        OTv = shb(4096, 6144).rearrange("p (m t) -> p m t", m=8)
        QR = [shb(6144 + i * 512, 6144 + (i + 1) * 512) for i in range(2)]
        RT = [[shf(7168 + (4 * i + q) * 256, 7168 + (4 * i + q + 1) * 256) for q in range(4)] for i in range(2)]
        Pt = [[shb(9216 + (2 * i + kb) * 256, 9216 + (2 * i + kb + 1) * 256) for kb in range(2)] for i in range(2)]
        Rt = [shf(10240 + i * 512, 10240 + (i + 1) * 512) for i in range(2)]
        KR = shb(11264, 11520)

        def gcol(idx, k):
            c = GV0 + idx * 8 + k
            return CF[:, c:c + 1]

        def cwcol(l, q, half, j):
            c = CW0 + ((l * 4 + q) * 2 + half) * NFP + j
            return CF[:, c:c + 1]

        sem_c = [self.newsem(f"s_const{i}") for i in range(3)]
        sem_x = [self.newsem(f"s_x{i}") for i in range(2)]
        sem_o = [self.newsem(f"s_o{i}") for i in range(2)]
        sem_sl = [self.newsem(f"s_sl{i}") for i in range(NSLOT)]
        sem_cv = [self.newsem(f"s_cv{i}") for i in range(len(CRANGES))]

        self.dma("sp", CF[:], cfd, sem_c[0], [], ["CF"])
        self.dma("sp", CB[:], cbd, sem_c[1], [], ["CBF"])
        self.dma("sp", POS[:], posd, sem_c[2], [], ["POS"])
        xload_ops = {}

        def xload(c):
            par = c % 2
            self.dma("pool", Xv[par], xTv[:, :, c * T:(c + 1) * T], sem_x[par], [],
                     [f"X{par}_{k}" for k in range(KD)])
        xload(0)
        for i, (a, b) in enumerate(CRANGES):
            self.dma("pool", wb[:, a:b], wsrc[:, a:b], sem_cv[i], [], [f"WB{i}"])
        self.memset("pool", VR[:], 0.0, [f"V{s}" for s in range(8)])
        self.memset("pool", UH[:], 0.0, [f"UH{l}_{j}_{h}" for l in range(4) for j in range(NFP) for h in range(2)])
        self.memset("pool", HPH[:], 0.0, ["HPH0", "HPH1"])
        POSF = shf(2048, 2080)
        ANG = shf(0, 1024).rearrange("p (b j) -> p b j", j=32)
        TMP = shf(1024, 2048)
        self.cp("dve", POSF, POS[:], ["POS"], ["S_POSF"])
        self.tt("dve", ANG, POSF.unsqueeze(2).to_broadcast([128, 32, 32]),
                CF[:, IF0:IF0 + 32].unsqueeze(1).to_broadcast([128, 32, 32]), ALU.mult,
                ["S_POSF", "CF"], ["S_ANG"])
        two_pi = 2.0 * math.pi
        C1 = 6.28125
        C2 = two_pi - C1
        ANGf = shf(0, 1024)
        TI = shf(2080, 3104).bitcast(I32)
        TF = shf(3104, 4128)
        for name, dst, shift in (("SIN", SIN, math.pi), ("COS", COS, 1.5 * math.pi)):
            self.ts("dve", TMP, ANGf, 1.0 / two_pi, ALU.mult, ["S_ANG"], ["S_TMP"], s2=shift / two_pi, op1=ALU.add)
            self.cp("dve", TI, TMP, ["S_TMP"], ["S_TI"])
            self.cp("dve", TF, TI, ["S_TI"], ["S_TF"])
            self.stt(TMP, TF, -C1, ANGf, ALU.mult, ALU.add, ["S_TF", "S_ANG", "S_TMP"], ["S_TMP"])
            self.stt(TMP, TF, -C2, TMP, ALU.mult, ALU.add, ["S_TF", "S_TMP"], ["S_TMP"])
            self.ts("dve", TMP, TMP, shift, ALU.add, ["S_TMP"], ["S_TMP"])
            self.ts("dve", TF, TMP, 0.0, ALU.is_lt, ["S_TMP", "S_TF"], ["S_TF"], s2=two_pi, op1=ALU.mult)
            self.tt("dve", TMP, TMP, TF, ALU.add, ["S_TMP", "S_TF"], ["S_TMP"])
            self.ts("dve", TF, TMP, two_pi, ALU.is_ge, ["S_TMP", "S_TF"], ["S_TF"], s2=-two_pi, op1=ALU.mult)
            self.tt("dve", TMP, TMP, TF, ALU.add, ["S_TMP", "S_TF"], ["S_TMP"])
            self.ts("dve", TMP, TMP, -math.pi, ALU.add, ["S_TMP"], ["S_TMP"])
            self.ts("dve", TMP, TMP, math.pi, ALU.min, ["S_TMP"], ["S_TMP"], s2=-math.pi, op1=ALU.max)
            self.act(dst[:], TMP, AF.Sin, ["S_TMP"], [name])
        self.act(ESK[:], CF[:, SK0:SK0 + 16], AF.Exp, ["CF"], ["ESK"])
        self.cp("dve", MK[:, 0:1], COS[:, 0:1], ["COS", "SIN", "ESK"], ["CHUNKDONE"])

        self.slab_n = 0

        def slab(key):
            o, n = OFFS[key]
            i = self.slab_n % NSLOT
            self.slab_n += 1
            self.dma("sp", SL[i][:, 0:n], wb[:, o:o + n], sem_sl[i], [f"WB{range_of(key)}"], [f"SL{i}"])
            return SL[i], f"SL{i}"

        def norm_stats(src_view, src_keys, bank):
            for k in range(KD):
                sq = SQ[:, (k % 2) * T:(k % 2 + 1) * T]
                self.act(sq, src_view[:, k, :], AF.Square, [src_keys[k]], [f"SQ{k % 2}"])
                self.mm(PS[bank][:], ONES, sq, k == 0, k == KD - 1, [f"SQ{k % 2}", "CBF"], [f"PS{bank}"])
            rstd_from(bank)

        def rstd_from(bank):
            self.act(RTMP[:], PS[bank][:], AF.Sqrt, [f"PS{bank}", "CF"], ["RTMP"], scale=1.0 / DM,
                     bias=CF[:, EP0:EP0 + 1])
            self.recip(RSTD[:], RTMP[:], ["RTMP"], ["RSTD"])

        def norm_apply(dst_view, dst_keys, par, gidx, extra_r=()):
            for k in range(KD):
                self.stt(dst_view(k), Xv[par][:, k, :], gcol(gidx, k), RSTD[:], ALU.mult, ALU.mult,
                         [f"X{par}_{k}", "RSTD", "CF"] + list(extra_r), [dst_keys[k]])

        def post_norm_residual(par, gidx, bank):
            for k in range(KD):
                self.stt(Fv[:, k, :], Fv[:, k, :], gcol(gidx, k), RSTD[:], ALU.mult, ALU.mult,
                         [f"F{k}", "RSTD", "CF"], [f"F{k}"])
                self.tt("dve" if k in (3, 7) else "pool", Xv[par][:, k, :], Xv[par][:, k, :], Fv[:, k, :], ALU.add,
                        [f"F{k}", f"X{par}_{k}"], [f"X{par}_{k}"])

        def pool_mixer(c, l, par):
            norm_stats(Xv[par], [f"X{par}_{k}" for k in range(KD)], 6)
            norm_apply(lambda k: HPv[:, k, 16:528], [f"HP{k}" for k in range(KD)], par, l, extra_r=["CHUNKDONE"])
            hpk = [f"HP{k}" for k in range(KD)]
            self.cp("pool", HPv[:, :, 0:16], HPHv[:, l, :, :], [f"HPH{l}"] + hpk, hpk)
            for k in range(KD):
                gi = k // 2
                eng = "dve" if k % 2 == 0 else "pool"
                t1, t2 = (PT[0], PT[1]) if eng == "dve" else (PT[2], PT[3])
                tk = ("PT0", "PT1") if eng == "dve" else ("PT2", "PT3")
                src, srck = HPv[:, k, :], f"HP{k}"
                lo = 0
                dst, dstk = t1, tk[0]
                for step in range(gi + 1):
                    sh = 1 << step
                    nlo = lo + sh
                    self.tt(eng, dst[:, nlo:528], src[:, nlo:528], src[:, lo:528 - sh], ALU.add, [srck], [dstk])
                    src, srck = dst, dstk
                    lo = nlo
                    dst, dstk = (t2, tk[1]) if dst is t1 else (t1, tk[0])
                w = POOL_W[gi]
                self.stt(Dv[:, k, :], src[:, 16:528], 1.0 / w, HPv[:, k, 16:528], ALU.mult, ALU.subtract,
                         [srck, f"HP{k}"], [f"Dd{k}"])
                if c == 0:
                    ic = CF[:, IC0 + gi * 16:IC0 + gi * 16 + 16]
                    self.tt("dve", src[:, 16:32], src[:, 16:32], ic, ALU.mult, [srck, "CF"], [srck])
                    self.tt("dve", Dv[:, k, 0:16], src[:, 16:32], HPv[:, k, 16:32], ALU.subtract,
                            [srck, f"HP{k}"], [f"Dd{k}"])
            self.cp("pool", HPHv[:, l, :, :], HPv[:, :, 512:528], hpk, [f"HPH{l}"])
            sl, slk = slab(f"pool{l}")
            wv = sl[:, 0:2048].rearrange("p (g kk c) -> p g kk c", g=4, kk=2)
            for ko in range(KD):
                gi, mm_ = ko // 2, ko % 2
                bank = ko % 4
                for kk in range(2):
                    self.mm(PS[bank][:], wv[:, gi, kk, mm_ * 128:(mm_ + 1) * 128], Dv[:, 2 * gi + kk, :],
                            kk == 0, kk == 1, [slk, f"Dd{2 * gi + kk}"], [f"PS{bank}"])
                self.act(Fv[:, ko, :], PS[bank][:], AF.Identity, [f"PS{bank}", "CF"], [f"F{ko}"],
                         scale=gcol(17 + l, ko))
                sq = SQ[:, (ko % 2) * T:(ko % 2 + 1) * T]
                self.act(sq, PS[bank][:], AF.Square, [f"PS{bank}", "CF"], [f"SQ{ko % 2}"], scale=gcol(17 + l, ko))
                self.mm(PS[6][:], ONES, sq, ko == 0, ko == KD - 1, [f"SQ{ko % 2}", "CBF"], ["PS6"])
            rstd_from(6)
            post_norm_residual(par, 4 + l, 6)

        def ffn(c, l, par):
            norm_stats(Xv[par], [f"X{par}_{k}" for k in range(KD)], 6)
            norm_apply(lambda k: Hv[:, k, :], [f"H{k}" for k in range(KD)], par, 8 + l)
            gpend = None
            for s in range(11):
                sl, slk = slab(f"win{l}_{s}")
                wv = sl[:].rearrange("p (fp h k c) -> p fp h k c", fp=2, h=2, k=KD)
                for fp in range(2):
                    j = 2 * s + fp
                    jb = j % 2
                    for half in range(2):
                        bank = 2 * jb + half
                        for k in range(KD):
                            self.mm(PS[bank][:], wv[:, fp, half, k, :], Hv[:, k, :], k == 0, k == KD - 1,
                                    [slk, f"H{k}"], [f"PS{bank}"])
                    for half in range(2):
                        bank = 2 * jb + half
                        U, uk = Ut[half][jb], f"U{half}{jb}"
                        Aa, ak = At[half][jb], f"A{half}{jb}"
                        hk = f"UH{l}_{j}_{half}"
                        self.cp("act", U[:, 2:514], PS[bank][:], [f"PS{bank}"], [uk])
                        self.cp("pool", U[:, 0:2], UHv[:, l, j, half, :], [hk, uk], [uk])
                        self.act(Aa, PS[bank][:], AF.Identity, [f"PS{bank}", "CF"], [ak],
                                 scale=cwcol(l, 2, half, j), bias=cwcol(l, 3, half, j))
                    for q, off in ((1, 1), (0, 0)):
                        for half in range(2):
                            U, uk = Ut[half][jb], f"U{half}{jb}"
                            Aa, ak = At[half][jb], f"A{half}{jb}"
                            self.stt(Aa, U[:, off:off + 512], cwcol(l, q, half, j), Aa, ALU.mult, ALU.add,
                                     [uk, ak, "CF"], [ak])
                    for half in range(2):
                        U, uk = Ut[half][jb], f"U{half}{jb}"
                        self.cp("pool", UHv[:, l, j, half, :], U[:, 512:514], [uk], [f"UH{l}_{j}_{half}"])
                    if gpend is not None:
                        pj, pjb = gpend
                        self.act(GE[pjb], At[0][pjb], AF.Gelu_apprx_tanh, [f"A0{pjb}"], [f"GE{pjb}"])
                        self.tt("pool", Gv[:, pj, :], GE[pjb], At[1][pjb], ALU.mult, [f"GE{pjb}", f"A1{pjb}"], [f"G{pj}"])
                    gpend = (j, jb)
            pj, pjb = gpend
            self.act(GE[pjb], At[0][pjb], AF.Gelu_apprx_tanh, [f"A0{pjb}"], [f"GE{pjb}"])
            self.tt("pool", Gv[:, pj, :], GE[pjb], At[1][pjb], ALU.mult, [f"GE{pjb}", f"A1{pjb}"], [f"G{pj}"])
            pend = None
            for m in range(8):
                sl, slk = slab(f"wout{l}_{m}")
                wv = sl[:, 0:FF].rearrange("p (j c) -> p j c", j=NFP)
                bank = 4 + m % 2
                for j in range(NFP):
                    self.mm(PS[bank][:], wv[:, j, :], Gv[:, j, :], j == 0, j == NFP - 1, [slk, f"G{j}"], [f"PS{bank}"])
                if pend is not None:
                    self.mm(PS[6][:], ONES, pend[0], pend[1] == 0, False, [pend[2], "CBF"], ["PS6"])
                self.cp("act", Fv[:, m, :], PS[bank][:], [f"PS{bank}"], [f"F{m}"])
                sq = SQ[:, (m % 2) * T:(m % 2 + 1) * T]
                self.act(sq, PS[bank][:], AF.Square, [f"PS{bank}"], [f"SQ{m % 2}"])
                pend = (sq, m, f"SQ{m % 2}")
            self.mm(PS[6][:], ONES, pend[0], False, True, [pend[2], "CBF"], ["PS6"])
            rstd_from(6)
            post_norm_residual(par, 12 + l, 6)

        def rope(psv, pskey, nh, blk, outv, outkey, rset):
            cb = COSv[:, blk, :].unsqueeze(1).to_broadcast([128, nh, 32])
            sb = SINv[:, blk, :].unsqueeze(1).to_broadcast([128, nh, 32])
            x1, x2 = psv[:, :, 0:32], psv[:, :, 32:64]
            r = [RT[rset][q][:, 0:nh * 32].rearrange("p (h j) -> p h j", j=32) for q in range(4)]
            rk = [f"RT{rset}{q}" for q in range(4)]
            self.tt("dve", r[0], x1, cb, ALU.mult, [pskey, "COS"], [rk[0]])
            self.tt("dve", r[1], x2, sb, ALU.mult, [pskey, "SIN"], [rk[1]])
            self.tt("dve", r[2], x2, cb, ALU.mult, [pskey, "COS"], [rk[2]])
            self.tt("dve", r[3], x1, sb, ALU.mult, [pskey, "SIN"], [rk[3]])
            self.tt("pool", outv[:, :, 0:32], r[0], r[1], ALU.subtract, [rk[0], rk[1]], [outkey])
            self.tt("pool", outv[:, :, 32:64], r[2], r[3], ALU.add, [rk[2], rk[3]], [outkey])

        def kv_step(c, par):
            norm_stats(Xv[par], [f"X{par}_{k}" for k in range(KD)], 6)
            norm_apply(lambda k: Hv[:, k, :], [f"H{k}" for k in range(KD)], par, 2)
            norm_apply(lambda k: HKVv[:, k, :], [f"HKV{k}" for k in range(KD)], par, 16)
            slk_, slkk = slab("kvk")
            slv_, slvk = slab("kvv")
            wk = slk_[:].rearrange("p (k c) -> p k c", k=KD)
            wvv = slv_[:, 0:2048].rearrange("p (k c) -> p k c", k=KD)
            for b in range(4):
                slot = 4 * (c % 2) + b
                bk, bv = 2 * (b % 2), 2 * (b % 2) + 1
                for k in range(KD):
                    self.mm(PS[bk][:], HKVv[:, k, b * 128:(b + 1) * 128], wk[:, k, :], k == 0, k == KD - 1,
                            [slkk, f"HKV{k}"], [f"PS{bk}"])
                for k in range(KD):
                    self.mm(PS[bv][:, 0:256], HKVv[:, k, b * 128:(b + 1) * 128], wvv[:, k, :], k == 0, k == KD - 1,
                            [slvk, f"HKV{k}"], [f"PS{bv}"])
                rope(PS[bk][:].rearrange("p (h d) -> p h d", d=64), f"PS{bk}", 8, 4 * c + b,
                     KR.rearrange("p (h d) -> p h d", d=64), "KR", b % 2)
                for g in range(4):
                    self.tr(PSB[4][:, g * 128:(g + 1) * 128], KR[:, g * 128:(g + 1) * 128], ["KR"], ["PS4"])
                self.cp("act", KKv[:, slot, :, :], PSB[4][:, 0:512].rearrange("p (g c) -> p g c", g=4),
                        ["PS4"], [f"KK{slot}"])
                vps = PS[bv][:, 0:256].rearrange("p (g d) -> p g d", g=4)
                self.cp("act", VRv[:, slot, :, 0, 0:64], vps, [f"PS{bv}"], [f"V{slot}"])
                self.cp("dve", VRv[:, slot, :, 1, 64:128], vps, [f"PS{bv}"], [f"V{slot}"])

        def attn_mixer(c, l, par):
            j = l - 2
            if l == 3:
                norm_stats(Xv[par], [f"X{par}_{k}" for k in range(KD)], 6)
                norm_apply(lambda k: Hv[:, k, :], [f"H{k}" for k in range(KD)], par, l)
            wq = []
            for hf in range(2):
                sl, slk = slab(f"wq{l}_{hf}")
                wq.append((sl[:].rearrange("p (k c) -> p k c", k=KD), slk))
            for b in range(4):
                qr = QR[b % 2]
                for hf in range(2):
                    bank = 2 * (b % 2) + hf
                    for k in range(KD):
                        self.mm(PS[bank][:], Hv[:, k, b * 128:(b + 1) * 128], wq[hf][0][:, k, :], k == 0, k == KD - 1,
                                [wq[hf][1], f"H{k}"], [f"PS{bank}"])
                    rope(PS[bank][:].rearrange("p (h d) -> p h d", d=64), f"PS{bank}", 8, 4 * c + b,
                         qr[:, hf * 512:(hf + 1) * 512].rearrange("p (h d) -> p h d", d=64), f"QR{b % 2}", hf)
                for m in range(8):
                    self.tr(PSB[4][:, m * 128:(m + 1) * 128], qr[:, m * 128:(m + 1) * 128], [f"QR{b % 2}"], ["PS4"])
                self.cp("act", QTv[:, :, b * 128:(b + 1) * 128], PSB[4][:].rearrange("p (m t) -> p m t", m=8),
                        ["PS4"], [f"QT{m}" for m in range(8)])
            if DBG_LEVEL <= 2:
                return
            steps = [(g, b) for g in range(4) for b in range(4)]

            def scores(i):
                g, b = steps[i]
                blk = 4 * c + b
                kbs = ["prev", "own"] if blk > 0 else ["own"]
                pb = i % 2
                nk = len(kbs)
                for kbi, kb in enumerate(kbs):
                    slot = (4 * (c % 2) + b - (1 if kb == "prev" else 0)) % 8
                    for ii in range(4):
                        h = 4 * g + ii
                        m, r0 = h // 2, 64 * (h % 2)
                        bank = 2 * pb + (ii % 2)
                        col = (kbi * 2 + ii // 2) * 128
                        self.mm(PS[bank][:, col:col + 128], KKv[r0:r0 + 64, slot, g, :],
                                QTv[r0:r0 + 64, m, b * 128:(b + 1) * 128], True, True,
                                [f"KK{slot}", f"QT{m}"], [f"PS{bank}"])
                for par_ in range(2):
                    bank = 2 * pb + par_
                    P, pk = Pt[pb][par_], f"P{pb}{par_}"
                    self.act(P[:, 0:nk * 256], PS[bank][:, 0:nk * 256], AF.Exp, [f"PS{bank}"], [pk], scale=0.125)
                    if nk == 2:
                        self.tt("pool", P, P, MASKPO, ALU.mult, [pk, "CBF"], [pk])
                    else:
                        self.tt("pool", P[:, 0:256], P[:, 0:256], MASK["own"][:, 0:256], ALU.mult, [pk, "CBF"], [pk])
                return kbs

            def pv(i, kbs):
                g, b = steps[i]
                pb = i % 2
                for pm in range(2):
                    for which, bank0 in (("V", 4), ("D", 6)):
                        bank = bank0 + pm
                        seq = [(hh, kbi) for hh in range(2) for kbi in range(len(kbs))]
                        for idx, (hh, kbi) in enumerate(seq):
                            slot = (4 * (c % 2) + b - (1 if kbs[kbi] == "prev" else 0)) % 8
                            lhsT = VRv[:, slot, g, hh, :] if which == "V" else ONEAB[hh]
                            rk = [f"V{slot}"] if which == "V" else ["CBF"]
                            col = (kbi * 2 + pm) * 128
                            self.mm(PS[bank][:, b * 128:(b + 1) * 128], lhsT,
                                    Pt[pb][hh][:, col:col + 128], idx == 0, idx == len(seq) - 1,
                                    rk + [f"P{pb}{hh}"], [f"PS{bank}"])
                if b == 3:
                    for pm in range(2):
                        m = 2 * g + pm
                        R, rk = Rt[pm], f"R{pm}"
                        self.ts("dve", R, PS[6 + pm][:], ESK[:, j * 8 + m:j * 8 + m + 1], ALU.add,
                                [f"PS{6 + pm}", "ESK"], [rk])
                        self.recip(R, R, [rk], [rk])
                        self.tt("dve", OTv[:, m, :], PS[4 + pm][:], R, ALU.mult, [f"PS{4 + pm}", rk], [f"OT{m}"])

            prev = None
            for i in range(len(steps)):
                kbs = scores(i)
                if prev is not None:
                    pv(i - 1, prev)
                prev = kbs
            pv(len(steps) - 1, prev)
            if DBG_LEVEL <= 3:
                return
            wo = []
            for hf in range(2):
                sl, slk = slab(f"wo{l}_{hf}")
                wo.append((sl[:].rearrange("p (m c) -> p m c", m=8), slk))
            pend = None
            for mo in range(8):
                hf = mo // 4
                bank = mo % 4
                for m in range(8):
                    self.mm(PS[bank][:], wo[hf][0][:, m, (mo % 4) * 128:(mo % 4 + 1) * 128], OTv[:, m, :],
                            m == 0, m == 7, [wo[hf][1], f"OT{m}"], [f"PS{bank}"])
                if pend is not None:
                    self.mm(PS[6][:], ONES, pend[0], pend[1] == 0, False, [pend[2], "CBF"], ["PS6"])
                self.cp("act", Fv[:, mo, :], PS[bank][:], [f"PS{bank}"], [f"F{mo}"])
                sq = SQ[:, (mo % 2) * T:(mo % 2 + 1) * T]
                self.act(sq, PS[bank][:], AF.Square, [f"PS{bank}"], [f"SQ{mo % 2}"])
                pend = (sq, mo, f"SQ{mo % 2}")
            self.mm(PS[6][:], ONES, pend[0], False, True, [pend[2], "CBF"], ["PS6"])
            rstd_from(6)
            post_norm_residual(par, 4 + l, 6)

        store_ops = []
        for c in range(NCH_RUN):
            par = c % 2
            for l in range(NL_RUN):
                if l < 2:
                    pool_mixer(c, l, par)
                else:
                    if l == 2:
                        kv_step(c, par)
                    if DBG_LEVEL <= 1:
                        break
                    attn_mixer(c, l, par)
                    if DBG_LEVEL <= 3:
                        break
                if l == 0 and c + 1 < NCH_RUN:
                    xload(c + 1)
                ffn(c, l, par)
            xk = [f"X{par}_{k}" for k in range(KD)]
            self.cp("dve", MK[:, 0:1], Xv[par][:, 0, 0:1], xk, ["CHUNKDONE"])
            store_ops.append(self.dma("pool", oTv[:, :, c * T:(c + 1) * T], Xv[par], sem_o[par], xk, []))
        fin = self.s.add("pool", lambda e: None, [], [])
        fin.deps = store_ops[-2:] if len(store_ops) >= 2 else store_ops[-1:]

    def emit(self):
        nc = self.nc
        s = self.s
        engsem = {e: f"s_{e}" for e in ("pe", "act", "dve", "pool", "sp")}
        s.finalize(engsem)
        with ExitStack() as es:
            semh = {}
            for n in list(engsem.values()) + self.semnames:
                semh[n] = es.enter_context(nc.semaphore(n))
            block = es.enter_context(nc.Block())

            @block.tensor
            def _(e):
                s.run("pe", e, semh)

            @block.scalar
            def _(e):
                s.run("act", e, semh)

            @block.vector
            def _(e):
                s.run("dve", e, semh)

            @block.gpsimd
            def _(e):
                s.run("pool", e, semh)

            @block.sync
            def _(e):
                s.run("sp", e, semh)


def build_nc():
    nc = bass.Bass("TRN2", target_bir_lowering=False)
    p = Prog(nc)
    p.build()
    p.emit()
    return nc


def kernel(**inputs):
    x = np.asarray(inputs["x"], np.float32)
    pos = np.asarray(inputs["positions"]).astype(np.int32)
    W = pack_weights(inputs)
    cf, cb = pack_consts(inputs)
    nc = build_nc()
    in_maps = []
    for b in range(NCORE):
        in_maps.append({
            "xT": np.ascontiguousarray(x[b].T),
            "wsrc": W,
            "cf": cf,
            "cb": cb,
            "pos": np.ascontiguousarray(pos[b].reshape(32, 128).T),
        })
    res = run_bass_kernel_spmd(nc, in_maps, core_ids=list(range(NCORE)))
    out = np.stack([np.asarray(r["outT"]).T for r in res.results], axis=0)
    return np.ascontiguousarray(out.astype(np.float32))
```
